# Optimizing a Trainium2 kernel written in Bass

```python
import math
import jax, jax.numpy as jnp
from jax import lax
import numpy as np

D_MODEL = 1024
BATCH = 8
SEQ = 4096
DEPTH = 1

POOL_WINDOWS = (2, 4, 8, 16)
POOL_GROUPS = len(POOL_WINDOWS)
POOL_WIDTH = D_MODEL // 2
POOL_GROUP_DIM = POOL_WIDTH // POOL_GROUPS
RNN_WIDTH = D_MODEL
RNN_HEADS = 16
RNN_HEAD_DIM = RNN_WIDTH // RNN_HEADS
RNN_CONV = 4
LRU_C = 8.0
D_FF = 3 * D_MODEL
FFN_CONV = 3
PLE_DIM = 256
N_BRANCH = 2
RMS_EPS = 1e-6
IN_POOL = POOL_WIDTH
IN_RNN_X = RNN_WIDTH
IN_RNN_G = RNN_WIDTH
IN_GATES = N_BRANCH * D_MODEL
IN_TOTAL = IN_POOL + IN_RNN_X + IN_RNN_G + IN_GATES

kernel_name = "hybrid_pool_rglru_gated_block"


def rms_norm(x, g):
    xf = x.astype(jnp.float32)
    y = xf * lax.rsqrt(jnp.mean(xf * xf, axis=-1, keepdims=True) + RMS_EPS) * g.astype(jnp.float32)
    return y.astype(x.dtype)


def causal_dwconv(u, w, b):
    k_w = w.shape[0]
    s = u.shape[1]
    up = jnp.pad(u, ((0, 0), (k_w - 1, 0), (0, 0)))
    out = b + up[:, 0:s] * w[0]
    for k in range(1, k_w):
        out = out + up[:, k:k + s] * w[k]
    return out


def pool_mixer(u, pool_w, pool_scale):
    b, s, _ = u.shape
    uf = u.astype(jnp.float32)
    cs = jnp.pad(jnp.cumsum(uf, axis=1), ((0, 0), (1, 0), (0, 0)))
    t = jnp.arange(s, dtype=jnp.int32)
    outs = []
    for g, w in enumerate(POOL_WINDOWS):
        sl = cs[..., g * POOL_GROUP_DIM:(g + 1) * POOL_GROUP_DIM]
        prev = jnp.pad(sl, ((0, 0), (w - 1, 0), (0, 0)))[:, :s]
        count = jnp.minimum(t + 1, w).astype(jnp.float32)[None, :, None]
        outs.append((sl[:, 1:] - prev) / count)
    mean = jnp.concatenate(outs, axis=-1)
    d = (mean - uf).astype(u.dtype).reshape(b, s, POOL_GROUPS, POOL_GROUP_DIM)
    y = jnp.einsum('bsgc,gcd->bsgd', d, pool_w).reshape(b, s, POOL_WIDTH)
    return y * pool_scale


def rg_lru_branch(xb, gb, conv_w, conv_b, w_gates, b_gates, lru_lambda):
    b, s, _ = xb.shape
    xc = causal_dwconv(xb, conv_w, conv_b)
    xh = xc.reshape(b, s, RNN_HEADS, RNN_HEAD_DIM)
    gates = jnp.einsum('bshi,ghij->gbshj', xh, w_gates).reshape(2, b, s, RNN_WIDTH)
    gates = gates.astype(jnp.float32) + b_gates.astype(jnp.float32)[:, None, None, :]
    r = jax.nn.sigmoid(gates[0])
    i = jax.nn.sigmoid(gates[1])
    log_a = -LRU_C * r * jax.nn.softplus(-lru_lambda.astype(jnp.float32))
    a = jnp.exp(log_a)
    mult = jnp.sqrt(-jnp.expm1(2.0 * log_a))
    t0 = (jnp.arange(s) == 0)[None, :, None]
    mult = jnp.where(t0, 1.0, mult)
    u = mult * i * xc.astype(jnp.float32)

    def combine(left, right):
        a_l, h_l = left
        a_r, h_r = right
        return a_l * a_r, a_r * h_l + h_r

    _, h = lax.associative_scan(combine, (a, u), axis=1)
    return h.astype(xb.dtype) * jax.nn.gelu(gb)


def setup_inputs(seed: int = 0) -> dict:
    key = jax.random.key(seed)
    ks = jax.random.split(key, 32)
    f32 = jnp.float32
    L = DEPTH

    def nrm(k, shape, fan_in):
        return jax.random.normal(k, shape, f32) * (fan_in ** -0.5)

    def gain(k, shape):
        return 1.0 + 0.05 * jax.random.normal(k, shape, f32)

    a_c = jax.random.uniform(ks[10], (L, RNN_WIDTH), f32, 0.9, 0.999)
    s_l = a_c ** (1.0 / LRU_C)
    lru_lambda = jnp.log(s_l) - jnp.log1p(-s_l)

    return {
        "x": jax.random.normal(ks[0], (BATCH, SEQ, D_MODEL), f32),
        "p": jax.random.normal(ks[1], (DEPTH, BATCH, SEQ, PLE_DIM), f32),
        "g_mix_pre": gain(ks[2], (L, D_MODEL)),
        "g_mix_post": gain(ks[3], (L, D_MODEL)),
        "w_in": nrm(ks[4], (L, D_MODEL, IN_TOTAL), D_MODEL),
        "pool_w": nrm(ks[5], (L, POOL_GROUPS, POOL_GROUP_DIM, POOL_GROUP_DIM), POOL_GROUP_DIM),
        "pool_scale": gain(ks[6], (L, POOL_WIDTH)),
        "w_pool_out": nrm(ks[7], (L, POOL_WIDTH, D_MODEL), POOL_WIDTH),
        "conv_w": nrm(ks[8], (L, RNN_CONV, RNN_WIDTH), RNN_CONV),
        "conv_b": 0.02 * jax.random.normal(ks[9], (L, RNN_WIDTH), f32),
        "w_rg_gates": nrm(ks[11], (L, 2, RNN_HEADS, RNN_HEAD_DIM, RNN_HEAD_DIM), RNN_HEAD_DIM),
        "b_rg_gates": 0.02 * jax.random.normal(ks[12], (L, 2, RNN_WIDTH), f32),
        "lru_lambda": lru_lambda,
        "w_rg_out": nrm(ks[13], (L, RNN_WIDTH, D_MODEL), RNN_WIDTH),
        "w_o": nrm(ks[14], (L, D_MODEL, D_MODEL), D_MODEL),
        "g_ffn_pre": gain(ks[15], (L, D_MODEL)),
        "g_ffn_post": gain(ks[16], (L, D_MODEL)),
        "w_up": nrm(ks[17], (L, D_MODEL, 2 * D_FF), D_MODEL),
        "ffn_conv_w": nrm(ks[18], (L, FFN_CONV, D_FF), FFN_CONV),
        "ffn_conv_b": 0.02 * jax.random.normal(ks[19], (L, D_FF), f32),
        "w_down": nrm(ks[20], (L, D_FF, D_MODEL), D_FF),
        "g_ple_gate": gain(ks[21], (L, D_MODEL)),
        "w_ple_gate": nrm(ks[22], (L, D_MODEL, D_MODEL), D_MODEL),
        "w_ple_proj": nrm(ks[23], (L, PLE_DIM, D_MODEL), PLE_DIM),
        "g_ple_post": gain(ks[24], (L, D_MODEL)),
    }


def reference(x, p, g_mix_pre, g_mix_post, w_in, pool_w, pool_scale, w_pool_out,
              conv_w, conv_b, w_rg_gates, b_rg_gates, lru_lambda, w_rg_out, w_o,
              g_ffn_pre, g_ffn_post, w_up, ffn_conv_w, ffn_conv_b, w_down,
              g_ple_gate, w_ple_gate, w_ple_proj, g_ple_post):
    for l in range(DEPTH):
        h = rms_norm(x, g_mix_pre[l])
        z = h @ w_in[l]
        c0 = IN_POOL
        c1 = c0 + IN_RNN_X
        c2 = c1 + IN_RNN_G
        u_pool = z[..., :c0]
        u_rx = z[..., c0:c1]
        u_rg = z[..., c1:c2]
        gate_pool = jax.nn.sigmoid(z[..., c2:c2 + D_MODEL])
        gate_rnn = jax.nn.sigmoid(z[..., c2 + D_MODEL:])

        y_pool = pool_mixer(u_pool, pool_w[l], pool_scale[l]) @ w_pool_out[l]
        y_rnn = rg_lru_branch(u_rx, u_rg, conv_w[l], conv_b[l], w_rg_gates[l],
                              b_rg_gates[l], lru_lambda[l]) @ w_rg_out[l]
        merged = gate_pool * y_pool + gate_rnn * y_rnn
        x = x + rms_norm(merged @ w_o[l], g_mix_post[l])

        h = rms_norm(x, g_ffn_pre[l])
        up = h @ w_up[l]
        gate_h = causal_dwconv(up[..., :D_FF], ffn_conv_w[l], ffn_conv_b[l])
        hid = jax.nn.gelu(gate_h) * up[..., D_FF:]
        x = x + rms_norm(hid @ w_down[l], g_ffn_post[l])

        ple_gate = jax.nn.sigmoid(rms_norm(x, g_ple_gate[l]) @ w_ple_gate[l])
        ple = rms_norm(p[l].astype(x.dtype) @ w_ple_proj[l], g_ple_post[l])
        x = x + ple_gate * ple
    return x
```

```python
import contextlib
import numpy as np
import concourse.bass as bass
import concourse.mybir as mybir
from concourse.bass_utils import run_bass_kernel_spmd

F32 = mybir.dt.float32
BF16 = mybir.dt.bfloat16
AF = mybir.ActivationFunctionType
ALU = mybir.AluOpType

D = 1024
S = 4096
T = 512
NT_TILES = S // T
NCH = D // 128
DFF = 3072
NJ = DFF // 128
NBLK = 33
NR = 4
NTMP = 10
EPS = 1e-6
SAME_ENGINE_SYNC = True

VOFF = {}
_o = 0
for _n, _k in [("g_mix_pre", 8), ("g_mix_post", 8), ("pool_scale", 4), ("conv_w", 32), ("conv_b", 8),
               ("b_r", 8), ("b_i", 8), ("lam", 8), ("g_ffn_pre", 8), ("g_ffn_post", 8),
               ("ffn_conv_w", 72), ("ffn_conv_b", 24), ("g_ple_gate", 8), ("g_ple_post", 8)]:
    VOFF[_n] = _o
    _o += _k
NV = _o


class Prog:
    def __init__(self):
        self.ops = []
        self.last_w = {}
        self.readers = {}
        self.chan_count = {}

    def add(self, eng, fn, reads=(), writes=(), chan=None):
        idx = len(self.ops)
        deps = set()
        for r in reads:
            w = self.last_w.get(r)
            if w is not None:
                deps.add(w)
        for k in writes:
            w = self.last_w.get(k)
            if w is not None:
                deps.add(w)
            for rd in self.readers.get(k, ()):
                deps.add(rd)
        for r in reads:
            self.readers.setdefault(r, []).append(idx)
        for k in writes:
            self.last_w[k] = idx
            self.readers[k] = []
        op = dict(eng=eng, fn=fn, chan=chan, deps=deps, marked=False, idx=idx, seq=0, chan_val=0)
        if chan is not None:
            self.chan_count[chan] = self.chan_count.get(chan, 0) + 16
            op["chan_val"] = self.chan_count[chan]
        self.ops.append(op)
        return idx

    def emit(self, nc, block, stack):
        ops = self.ops
        for op in ops:
            for d in list(op["deps"]):
                dop = ops[d]
                if dop["chan"] is not None:
                    continue
                if op["chan"] is None and dop["eng"] == op["eng"]:
                    if op["eng"] == "pe" or not SAME_ENGINE_SYNC:
                        op["deps"].discard(d)
                        continue
                dop["marked"] = True
        cnt = {}
        for op in ops:
            if op["chan"] is None and op["marked"]:
                cnt[op["eng"]] = cnt.get(op["eng"], 0) + 1
                op["seq"] = cnt[op["eng"]]
        esem = {e: stack.enter_context(nc.semaphore("e_" + e)) for e in ["pe", "act", "dve", "pool"]}
        csem = {c: stack.enter_context(nc.semaphore("c_" + c)) for c in self.chan_count}
        streams = {e: [op for op in ops if op["eng"] == e] for e in ["pe", "act", "dve", "pool", "sp"]}

        def run(engname, eng):
            waited = {}
            for op in streams[engname]:
                need = {}
                for d in op["deps"]:
                    dop = ops[d]
                    if dop["chan"] is not None:
                        key, sem, val = "c_" + dop["chan"], csem[dop["chan"]], dop["chan_val"]
                    else:
                        key, sem, val = "e_" + dop["eng"], esem[dop["eng"]], dop["seq"]
                    if val > need.get(key, (None, 0))[1]:
                        need[key] = (sem, val)
                for key, (sem, val) in need.items():
                    if val > waited.get(key, 0):
                        eng.wait_ge(sem, val)
                        waited[key] = val
                ins = op["fn"](eng)
                if ins is None:
                    continue
                if op["chan"] is not None:
                    ins.then_inc(csem[op["chan"]], 16)
                elif op["marked"]:
                    ins.then_inc(esem[engname], 1)

        @block.sync
        def _(e):
            run("sp", e)

        @block.gpsimd
        def _(e):
            run("pool", e)

        @block.scalar
        def _(e):
            run("act", e)

        @block.vector
        def _(e):
            run("dve", e)

        @block.tensor
        def _(e):
            run("pe", e)


def build_nc(ntiles=NT_TILES):
    nc = bass.Bass("TRN2", target_bir_lowering=False)
    xT = nc.dram_tensor("xT", [D, S], F32, kind="ExternalInput").ap()
    pT = nc.dram_tensor("pT", [256, S], F32, kind="ExternalInput").ap()
    vecs_d = nc.dram_tensor("vecs", [128, NV], F32, kind="ExternalInput").ap()
    wstream = nc.dram_tensor("wstream", [NBLK, 128, 4096], F32, kind="ExternalInput").ap()
    poolw_d = nc.dram_tensor("poolw", [128, 512], F32, kind="ExternalInput").ap()
    wpo_d = nc.dram_tensor("wpo", [128, 4096], F32, kind="ExternalInput").ap()
    wpp_d = nc.dram_tensor("wpp", [128, 2048], F32, kind="ExternalInput").ap()
    bd_d = nc.dram_tensor("bd", [128, 2048], F32, kind="ExternalInput").ap()
    scr = nc.dram_tensor("scr", [NBLK, 128, 4096], BF16, kind="Internal").ap()
    yT = nc.dram_tensor("yT", [D, S], F32, kind="ExternalOutput").ap()

    xT_v = xT.rearrange("(c p) t -> p c t", p=128)
    yT_v = yT.rearrange("(c p) t -> p c t", p=128)
    pT_v = pT.rearrange("(c p) t -> p c t", p=128)

    P = Prog()
    with contextlib.ExitStack() as st:
        def sb(name, shape, dt):
            return st.enter_context(nc.sbuf_tensor(name, shape, dt))

        xres = [sb(f"xres{i}", [128, NCH, T], F32) for i in range(2)]
        hbf = sb("hbf", [128, NCH, T], BF16)
        big = sb("big", [128, NJ, T], BF16)
        mos = sb("mos", [128, NCH, T], F32)
        tmp32 = [sb(f"t32_{i}", [128, T], F32) for i in range(NTMP)]
        sqb = [sb(f"sq{i}", [128, T], BF16) for i in range(2)]
        rstd_t = [sb(f"rstd{i}", [128, T], F32) for i in range(2)]
        xcb = [sb(f"xcb{i}", [128, T], BF16) for i in range(2)]
        pb = sb("pb", [128, 2, T], BF16)
        pf = [sb(f"pf{i}", [128, 2, T], F32) for i in range(1)]
        rawp = [sb(f"rawp{g}", [128, T + 16], F32) for g in range(2)]
        hp = sb("hp", [128, 4, 16], F32)
        sA = sb("sA", [128, T + 16], F32)
        sB = sb("sB", [128, T + 16], F32)
        rawx = [sb(f"rawx{i}", [128, T + 3], F32) for i in range(2)]
        rawg = [sb(f"rawg{i}", [128, T + 2], F32) for i in range(2)]
        hx = sb("hx", [128, NCH, 3], F32)
        hg2 = sb("hg2", [128, NJ, 2], F32)
        hstate = sb("hstate", [128, NCH], F32)
        vec = sb("vec", [128, NV], F32)
        dv = sb("dv", [128, 64], F32)
        invc = sb("invc", [128, 4, 16], F32)
        t16 = sb("t16", [128, 16], F32)
        ones = sb("ones", [128, 128], BF16)
        negh = sb("negh", [128, T], F32)
        halfc = sb("halfc", [128, T], F32)
        ring = [sb(f"ring{i}", [128, 8, 512], BF16) for i in range(NR)]
        poolw = sb("poolw_s", [128, 4, 128], BF16)
        wpo = sb("wpo_s", [128, 4, 1024], BF16)
        wpp = sb("wpp_s", [128, 2, 1024], BF16)
        bd = sb("bd_s", [128, 2, NCH, 128], BF16)
        psum = [st.enter_context(nc.psum_tensor(f"ps{i}", [128, T], F32)) for i in range(8)]
        NRM = 7

        state = {"bank": 0, "tmp": 0, "rstd": 0}

        def nbank():
            b = state["bank"]
            state["bank"] = (b + 1) % 7
            return b

        def ntmp():
            i = state["tmp"]
            state["tmp"] = (i + 1) % NTMP
            return i

        def V(name, col):
            o = VOFF[name] + col
            return vec[:, o:o + 1]

        DV = {"half_br": 0, "half_bi": 8, "c8": 16, "hc8": 24, "ps_half": 32, "e": 40, "sp": 48}

        def DVc(name, col):
            o = DV[name] + col
            return dv[:, o:o + 1]

        P.add("sp", lambda e: e.dma_start(out=vec[:], in_=vecs_d[:, :]), writes=["vec"], chan="vec")
        P.add("pool", lambda e: e.dma_start(out=poolw[:].rearrange("p a b -> p (a b)"), in_=poolw_d[:, :]),
              writes=["poolw"], chan="w_poolw")
        P.add("pool", lambda e: e.dma_start(out=wpo[:].rearrange("p a b -> p (a b)"), in_=wpo_d[:, :]),
              writes=["wpo"], chan="w_wpo")
        P.add("pool", lambda e: e.dma_start(out=wpp[:].rearrange("p a b -> p (a b)"), in_=wpp_d[:, :]),
              writes=["wpp"], chan="w_wpp")
        P.add("pool", lambda e: e.dma_start(out=bd[:].rearrange("p a b c -> p (a b c)"), in_=bd_d[:, :]),
              writes=["bd"], chan="w_bd")
        P.add("pool", lambda e: e.dma_start(out=xres[0][:], in_=xT_v[:, :, 0:T]),
              writes=[f"x0_{c}" for c in range(NCH)], chan="x0")
        P.add("pool", lambda e: e.dma_start(out=pf[0][:], in_=pT_v[:, :, 0:T]), writes=["pf0"], chan="p0")
        cast_groups = [(0, 3), (3, 7), (7, 13), (13, 19), (19, 25), (25, 29), (29, 33)]
        for gi, (a, b) in enumerate(cast_groups):
            P.add("pool", lambda e, a=a, b=b: e.dma_start(out=scr[a:b], in_=wstream[a:b]),
                  writes=[f"scr{i}" for i in range(a, b)], chan=f"cast{gi}")

        P.add("dve", lambda e: e.memset(ones[:], 1.0), writes=["ones"])
        P.add("dve", lambda e: e.memset(negh[:], -0.5), writes=["negh"])
        P.add("dve", lambda e: e.memset(halfc[:], 0.5), writes=["halfc"])
        P.add("dve", lambda e: e.memset(hstate[:], 0.0), writes=[f"hstate{c}" for c in range(NCH)])
        P.add("dve", lambda e: e.memset(hx[:].rearrange("p a b -> p (a b)"), 0.0), writes=[f"hx{c}" for c in range(NCH)])
        P.add("dve", lambda e: e.memset(hg2[:].rearrange("p a b -> p (a b)"), 0.0), writes=[f"hg2_{j}" for j in range(NJ)])
        P.add("dve", lambda e: e.memset(hp[:].rearrange("p a b -> p (a b)"), 0.0), writes=[f"hp{g}" for g in range(4)])
        for g, w in enumerate((2, 4, 8, 16)):
            P.add("dve", lambda e, g=g, w=w: e.memset(invc[:, g, :], 1.0 / w), writes=["invc"])
            for t in range(w - 1):
                P.add("dve", lambda e, g=g, t=t: e.memset(invc[:, g, t:t + 1], 1.0 / (t + 1)), writes=["invc"])
        P.add("dve", lambda e: e.tensor_scalar(out=dv[:, 0:8], in0=vec[:, VOFF["b_r"]:VOFF["b_r"] + 8],
                                               scalar1=0.5, scalar2=None, op0=ALU.mult),
              reads=["vec"], writes=["dv_br"])
        P.add("dve", lambda e: e.tensor_scalar(out=dv[:, 8:16], in0=vec[:, VOFF["b_i"]:VOFF["b_i"] + 8],
                                               scalar1=0.5, scalar2=None, op0=ALU.mult),
              reads=["vec"], writes=["dv_bi"])
        P.add("dve", lambda e: e.tensor_scalar(out=dv[:, 32:36], in0=vec[:, VOFF["pool_scale"]:VOFF["pool_scale"] + 4],
                                               scalar1=0.5, scalar2=None, op0=ALU.mult),
              reads=["vec"], writes=["dv_ps"])
        P.add("act", lambda e: e.activation(out=dv[:, 40:48], in_=vec[:, VOFF["lam"]:VOFF["lam"] + 8],
                                            func=AF.Exp, scale=-1.0),
              reads=["vec"], writes=["dv_e"])
        P.add("act", lambda e: e.activation(out=dv[:, 48:56], in_=dv[:, 40:48], func=AF.Ln, bias=1.0),
              reads=["dv_e"], writes=["dv_sp"])
        P.add("dve", lambda e: e.tensor_scalar(out=dv[:, 16:24], in0=dv[:, 48:56], scalar1=-8.0, scalar2=None,
                                               op0=ALU.mult), reads=["dv_sp"], writes=["dv_c8"])
        P.add("dve", lambda e: e.tensor_scalar(out=dv[:, 24:32], in0=dv[:, 48:56], scalar1=-4.0, scalar2=None,
                                               op0=ALU.mult), reads=["dv_sp"], writes=["dv_hc8"])
        DVKEYS = ["dv_br", "dv_bi", "dv_ps", "dv_c8", "dv_hc8"]

        total_blocks = ntiles * NBLK

        def load_block(gi):
            if gi >= total_blocks:
                return
            slot = gi % NR
            bi = gi % NBLK
            P.add("sp", lambda e, slot=slot, bi=bi: e.dma_start(
                out=ring[slot][:].rearrange("p a b -> p (a b)"), in_=scr[bi]),
                reads=[f"scr{bi}"], writes=[f"ring{slot}"], chan=f"ring{slot}")

        for gi in range(NR):
            load_block(gi)

        def wgroup(t, bi, col, rhs_fn, rhs_keys, nk=8, bank=None, start=True, stop=True, extra_reads=()):
            gi = t * NBLK + bi
            slot = gi % NR
            if bank is None:
                bank = nbank()

            def fn(e, slot=slot, bank=bank):
                ins = None
                for k in range(nk):
                    ins = e.matmul(psum[bank][:], ring[slot][:, k, col:col + 128], rhs_fn(k),
                                   start=(start and k == 0), stop=(stop and k == nk - 1))
                return ins
            P.add("pe", fn, reads=[f"ring{slot}"] + list(rhs_keys) + list(extra_reads), writes=[f"ps{bank}"])
            return bank

        def done_block(t, bi):
            load_block(t * NBLK + bi + NR)

        hbf_keys = [f"hbf{k}" for k in range(NCH)]

        def hbf_rhs(k):
            return hbf[:, k, :]

        def norm_stats_chunk(src_ap, src_key, c, n=NCH):
            b = c % 2
            P.add("act", lambda e, b=b: e.activation(out=sqb[b][:], in_=src_ap, func=AF.Square),
                  reads=[src_key], writes=[f"sq{b}"])
            P.add("pe", lambda e, b=b: e.matmul(psum[NRM][:], ones[:], sqb[b][:], start=(c == 0), stop=(c == n - 1)),
                  reads=["ones", f"sq{b}"], writes=[f"ps{NRM}"])

        def norm_rstd():
            i1 = ntmp()
            i2 = state["rstd"]
            state["rstd"] = 1 - i2
            P.add("dve", lambda e: e.tensor_scalar(out=tmp32[i1][:], in0=psum[NRM][:], scalar1=1.0 / D, scalar2=EPS,
                                                   op0=ALU.mult, op1=ALU.add),
                  reads=[f"ps{NRM}"], writes=[f"t{i1}"])
            P.add("pool", lambda e: e.tensor_tensor(out=rstd_t[i2][:], in0=tmp32[i1][:], in1=negh[:], op=ALU.pow),
                  reads=[f"t{i1}", "negh"], writes=[f"rstd{i2}"])
            return i2

        def prenorm(par, gname):
            for c in range(NCH):
                norm_stats_chunk(xres[par][:, c, :], f"x{par}_{c}", c)
            ir = norm_rstd()
            for c in range(NCH):
                P.add("dve", lambda e, c=c: e.scalar_tensor_tensor(
                    out=hbf[:, c, :], in0=xres[par][:, c, :], scalar=V(gname, c), in1=rstd_t[ir][:],
                    op0=ALU.mult, op1=ALU.mult),
                    reads=[f"x{par}_{c}", "vec", f"rstd{ir}"], writes=[f"hbf{c}"])

        def postnorm_resid(par, gname):
            ir = norm_rstd()
            for c in range(NCH):
                it = ntmp()
                P.add("dve", lambda e, c=c, it=it: e.scalar_tensor_tensor(
                    out=tmp32[it][:], in0=mos[:, c, :], scalar=V(gname, c), in1=rstd_t[ir][:],
                    op0=ALU.mult, op1=ALU.mult),
                    reads=[f"mos{c}", "vec", f"rstd{ir}"], writes=[f"t{it}"])
                P.add("pool", lambda e, c=c, it=it: e.tensor_tensor(
                    out=xres[par][:, c, :], in0=tmp32[it][:], in1=xres[par][:, c, :], op=ALU.add),
                    reads=[f"t{it}", f"x{par}_{c}"], writes=[f"x{par}_{c}"])

        def evac_with_stats(bank, c):
            norm_stats_chunk(psum[bank][:], f"ps{bank}", c)
            P.add("act", lambda e: e.activation(out=mos[:, c, :], in_=psum[bank][:], func=AF.Copy),
                  reads=[f"ps{bank}"], writes=[f"mos{c}"])

        xkeys = [[f"x{par}_{c}" for c in range(NCH)] for par in range(2)]

        def load_p(t):
            if t >= ntiles:
                return
            P.add("pool", lambda e: e.dma_start(out=pf[0][:], in_=pT_v[:, :, t * T:(t + 1) * T]),
                  writes=["pf0"], chan="p0")

        def load_x(t):
            if t >= ntiles:
                return
            par = t % 2
            P.add("pool", lambda e: e.dma_start(out=xres[par][:], in_=xT_v[:, :, t * T:(t + 1) * T]),
                  writes=xkeys[par], chan=f"x{par}")

        def emit_tile(t):
            par = t % 2
            HG, MP, MR = 0, 8, 16
            DD, PM = 16, 20

            prenorm(par, "g_mix_pre")
            load_x(t + 1)

            for g, w in enumerate((2, 4, 8, 16)):
                bank = wgroup(t, 0, g * 128, hbf_rhs, hbf_keys)
                rb_ = g % 2
                rk = f"rawp{rb_}"
                rp = rawp[rb_]
                P.add("pool", lambda e, g=g, rp=rp: e.tensor_copy(out=rp[:, 0:16], in_=hp[:, g, :]), reads=[f"hp{g}"], writes=[rk])
                P.add("act", lambda e, rp=rp, bank=bank: e.activation(out=rp[:, 16:16 + T], in_=psum[bank][:], func=AF.Copy),
                      reads=[f"ps{bank}"], writes=[rk])
                P.add("pool", lambda e, g=g, rp=rp: e.tensor_copy(out=hp[:, g, :], in_=rp[:, T:T + 16]), reads=[rk], writes=[f"hp{g}"])
                L = T + 16
                P.add("pool", lambda e, rp=rp: e.tensor_tensor(out=sA[:, 1:L], in0=rp[:, 1:L], in1=rp[:, 0:L - 1], op=ALU.add),
                      reads=[rk], writes=["sA"])
                last, lastk = sA, "sA"
                if w >= 4:
                    P.add("pool", lambda e: e.tensor_tensor(out=sB[:, 3:L], in0=sA[:, 3:L], in1=sA[:, 1:L - 2], op=ALU.add),
                          reads=["sA"], writes=["sB"])
                    last, lastk = sB, "sB"
                if w >= 8:
                    P.add("pool", lambda e: e.tensor_tensor(out=sA[:, 7:L], in0=sB[:, 7:L], in1=sB[:, 3:L - 4], op=ALU.add),
                          reads=["sB"], writes=["sA"])
                    last, lastk = sA, "sA"
                if w >= 16:
                    P.add("pool", lambda e: e.tensor_tensor(out=sB[:, 15:L], in0=sA[:, 15:L], in1=sA[:, 7:L - 8], op=ALU.add),
                          reads=["sA"], writes=["sB"])
                    last, lastk = sB, "sB"
                dk = f"big{DD + g}"
                P.add("dve", lambda e, g=g, w=w, last=last, rp=rp: e.scalar_tensor_tensor(
                    out=big[:, DD + g, :], in0=last[:, 16:16 + T], scalar=1.0 / w, in1=rp[:, 16:16 + T],
                    op0=ALU.mult, op1=ALU.subtract),
                    reads=[lastk, rk], writes=[dk])
                if t == 0:
                    P.add("dve", lambda e, g=g, last=last: e.tensor_tensor(out=t16[:], in0=last[:, 16:32], in1=invc[:, g, :], op=ALU.mult),
                          reads=[lastk, "invc", dk], writes=["t16"])
                    P.add("dve", lambda e, g=g, rp=rp: e.tensor_tensor(out=big[:, DD + g, 0:16], in0=t16[:], in1=rp[:, 16:32], op=ALU.subtract),
                          reads=["t16", rk], writes=[dk])
                bk2 = nbank()
                P.add("pe", lambda e, g=g, bk2=bk2: e.matmul(psum[bk2][:], poolw[:, g, :], big[:, DD + g, :], start=True, stop=True),
                      reads=["poolw", dk], writes=[f"ps{bk2}"])
                P.add("act", lambda e, g=g, bk2=bk2: e.activation(out=big[:, PM + g, :], in_=psum[bk2][:], func=AF.Identity,
                                                                 scale=DVc("ps_half", g)),
                      reads=[f"ps{bk2}", "dv_ps"], writes=[f"big{PM + g}"])
            done_block(t, 0)

            for c in range(NCH):
                bi = 1 + c // 4
                zb = wgroup(t, bi, (c % 4) * 128, hbf_rhs, hbf_keys)
                if c % 4 == 3:
                    done_block(t, bi)
                yb = nbank()

                def fn(e, c=c, yb=yb):
                    ins = None
                    for g in range(4):
                        ins = e.matmul(psum[yb][:], wpo[:, g, c * 128:(c + 1) * 128], big[:, PM + g, :],
                                       start=(g == 0), stop=(g == 3))
                    return ins
                P.add("pe", fn, reads=["wpo"] + [f"big{PM + g}" for g in range(4)], writes=[f"ps{yb}"])
                it = ntmp()
                P.add("act", lambda e, it=it, zb=zb: e.activation(out=tmp32[it][:], in_=psum[zb][:], func=AF.Tanh, scale=0.5),
                      reads=[f"ps{zb}"], writes=[f"t{it}"])
                P.add("dve", lambda e, c=c, it=it, yb=yb: e.scalar_tensor_tensor(
                    out=big[:, MP + c, :], in0=tmp32[it][:], scalar=1.0, in1=psum[yb][:], op0=ALU.add, op1=ALU.mult),
                    reads=[f"t{it}", f"ps{yb}"], writes=[f"big{MP + c}"])

            stA = {}

            def rnn_A(c):
                bi = 3 + c // 4
                ub = wgroup(t, bi, (c % 4) * 128, hbf_rhs, hbf_keys)
                if c % 4 == 3:
                    done_block(t, bi)
                b = c % 2
                rk = f"rawx{b}"
                P.add("pool", lambda e: e.tensor_copy(out=rawx[b][:, 0:3], in_=hx[:, c, :]), reads=[f"hx{c}"], writes=[rk])
                P.add("act", lambda e: e.activation(out=rawx[b][:, 3:3 + T], in_=psum[ub][:], func=AF.Copy),
                      reads=[f"ps{ub}"], writes=[rk])
                P.add("pool", lambda e: e.tensor_copy(out=hx[:, c, :], in_=rawx[b][:, T:T + 3]), reads=[rk], writes=[f"hx{c}"])
                ixc = ntmp()
                P.add("act", lambda e: e.activation(out=tmp32[ixc][:], in_=psum[ub][:], func=AF.Identity,
                                                    scale=V("conv_w", 3 * 8 + c), bias=V("conv_b", c)),
                      reads=[f"ps{ub}", "vec"], writes=[f"t{ixc}"])
                for k in range(3):
                    P.add("dve", lambda e, k=k: e.scalar_tensor_tensor(
                        out=tmp32[ixc][:], in0=rawx[b][:, k:k + T], scalar=V("conv_w", k * 8 + c), in1=tmp32[ixc][:],
                        op0=ALU.mult, op1=ALU.add),
                        reads=[rk, "vec", f"t{ixc}"], writes=[f"t{ixc}"])
                P.add("pool", lambda e: e.tensor_copy(out=xcb[b][:], in_=tmp32[ixc][:]), reads=[f"t{ixc}"], writes=[f"xcb{b}"])
                stA[c] = ixc

            def rnn_B(c):
                b = c % 2
                ixc = stA[c]
                rb = nbank()
                P.add("pe", lambda e: e.matmul(psum[rb][:], bd[:, 0, c, :], xcb[b][:], start=True, stop=True),
                      reads=["bd", f"xcb{b}"], writes=[f"ps{rb}"])
                ib = nbank()
                P.add("pe", lambda e: e.matmul(psum[ib][:], bd[:, 1, c, :], xcb[b][:], start=True, stop=True),
                      reads=["bd", f"xcb{b}"], writes=[f"ps{ib}"])
                itr, ia, ia2, iti = ntmp(), ntmp(), ntmp(), ntmp()
                P.add("act", lambda e: e.activation(out=tmp32[itr][:], in_=psum[rb][:], func=AF.Tanh, scale=0.5,
                                                    bias=DVc("half_br", c)),
                      reads=[f"ps{rb}", "dv_br"], writes=[f"t{itr}"])
                P.add("act", lambda e: e.activation(out=tmp32[ia][:], in_=tmp32[itr][:], func=AF.Exp,
                                                    scale=DVc("hc8", c), bias=DVc("hc8", c)),
                      reads=[f"t{itr}", "dv_hc8"], writes=[f"t{ia}"])
                P.add("act", lambda e: e.activation(out=tmp32[ia2][:], in_=tmp32[itr][:], func=AF.Exp,
                                                    scale=DVc("c8", c), bias=DVc("c8", c)),
                      reads=[f"t{itr}", "dv_c8"], writes=[f"t{ia2}"])
                P.add("act", lambda e: e.activation(out=tmp32[iti][:], in_=psum[ib][:], func=AF.Tanh, scale=0.5,
                                                    bias=DVc("half_bi", c)),
                      reads=[f"ps{ib}", "dv_bi"], writes=[f"t{iti}"])
                P.add("dve", lambda e: e.tensor_scalar(out=tmp32[ia2][:], in0=tmp32[ia2][:], scalar1=-1.0, scalar2=1.0,
                                                       op0=ALU.mult, op1=ALU.add),
                      reads=[f"t{ia2}"], writes=[f"t{ia2}"])
                P.add("pool", lambda e: e.tensor_tensor(out=tmp32[itr][:], in0=tmp32[ia2][:], in1=halfc[:], op=ALU.pow),
                      reads=[f"t{ia2}", "halfc", f"t{itr}"], writes=[f"t{itr}"])
                if t == 0:
                    P.add("pool", lambda e: e.memset(tmp32[itr][:, 0:1], 1.0), reads=[f"t{itr}"], writes=[f"t{itr}"])
                P.add("dve", lambda e: e.scalar_tensor_tensor(out=tmp32[iti][:], in0=tmp32[iti][:], scalar=1.0, in1=tmp32[ixc][:],
                                                              op0=ALU.add, op1=ALU.mult),
                      reads=[f"t{iti}", f"t{ixc}"], writes=[f"t{iti}"])
                P.add("dve", lambda e: e.scalar_tensor_tensor(out=tmp32[iti][:], in0=tmp32[iti][:], scalar=0.5, in1=tmp32[itr][:],
                                                              op0=ALU.mult, op1=ALU.mult),
                      reads=[f"t{iti}", f"t{itr}"], writes=[f"t{iti}"])
                P.add("dve", lambda e: e.tensor_tensor_scan(out=mos[:, c, :], data0=tmp32[ia][:], data1=tmp32[iti][:],
                                                            initial=hstate[:, c:c + 1], op0=ALU.mult, op1=ALU.add),
                      reads=[f"t{ia}", f"t{iti}", f"hstate{c}"], writes=[f"mos{c}"])
                P.add("pool", lambda e: e.tensor_copy(out=hstate[:, c:c + 1], in_=mos[:, c, T - 1:T]),
                      reads=[f"mos{c}"], writes=[f"hstate{c}"])

            for i in range(NCH + 1):
                if i < NCH:
                    rnn_A(i)
                if i >= 1:
                    rnn_B(i - 1)

            for c in range(NCH):
                bi = 5 + c // 4
                gb = wgroup(t, bi, (c % 4) * 128, hbf_rhs, hbf_keys)
                if c % 4 == 3:
                    done_block(t, bi)
                ig = ntmp()
                P.add("act", lambda e, ig=ig, gb=gb: e.activation(out=tmp32[ig][:], in_=psum[gb][:], func=AF.Gelu_apprx_tanh),
                      reads=[f"ps{gb}"], writes=[f"t{ig}"])
                P.add("dve", lambda e, c=c, ig=ig: e.scalar_tensor_tensor(
                    out=big[:, HG + c, :], in0=mos[:, c, :], scalar=0.5, in1=tmp32[ig][:], op0=ALU.mult, op1=ALU.mult),
                    reads=[f"mos{c}", f"t{ig}"], writes=[f"big{HG + c}"])

            hg_keys = [f"big{HG + k}" for k in range(NCH)]
            for c in range(NCH):
                bz = 8 + 2 * (c // 4)
                by = 7 + 2 * (c // 4)
                zb = wgroup(t, bz, (c % 4) * 128, hbf_rhs, hbf_keys)
                yb = wgroup(t, by, (c % 4) * 128, lambda k: big[:, HG + k, :], hg_keys)
                if c % 4 == 3:
                    done_block(t, by)
                    done_block(t, bz)
                it = ntmp()
                P.add("act", lambda e, it=it, zb=zb: e.activation(out=tmp32[it][:], in_=psum[zb][:], func=AF.Tanh, scale=0.5),
                      reads=[f"ps{zb}"], writes=[f"t{it}"])
                P.add("dve", lambda e, c=c, it=it, yb=yb: e.scalar_tensor_tensor(
                    out=big[:, MR + c, :], in0=tmp32[it][:], scalar=1.0, in1=psum[yb][:], op0=ALU.add, op1=ALU.mult),
                    reads=[f"t{it}", f"ps{yb}"], writes=[f"big{MR + c}"])

            mpr_keys = [f"big{MP + k}" for k in range(NCH)] + [f"big{MR + k}" for k in range(NCH)]
            for c in range(NCH):
                bi = 11 + c // 4
                ob = wgroup(t, bi, (c % 4) * 128, lambda k: big[:, MP + k, :], mpr_keys, start=True, stop=False)
                wgroup(t, bi, (c % 4) * 128, lambda k: big[:, MR + k, :], mpr_keys, bank=ob, start=False, stop=True)
                if c % 4 == 3:
                    done_block(t, bi)
                evac_with_stats(ob, c)
            postnorm_resid(par, "g_mix_post")

            prenorm(par, "g_ffn_pre")
            for j in range(NJ):
                q, jj = j // 4, j % 4
                bg = 13 + 2 * q
                bu = 14 + 2 * q
                gb = wgroup(t, bg, jj * 128, hbf_rhs, hbf_keys)
                ub = wgroup(t, bu, jj * 128, hbf_rhs, hbf_keys)
                if jj == 3:
                    done_block(t, bg)
                    done_block(t, bu)
                b = j % 2
                rk = f"rawg{b}"
                P.add("pool", lambda e, j=j, b=b: e.tensor_copy(out=rawg[b][:, 0:2], in_=hg2[:, j, :]), reads=[f"hg2_{j}"], writes=[rk])
                P.add("act", lambda e, b=b, gb=gb: e.activation(out=rawg[b][:, 2:2 + T], in_=psum[gb][:], func=AF.Copy),
                      reads=[f"ps{gb}"], writes=[rk])
                P.add("pool", lambda e, j=j, b=b: e.tensor_copy(out=hg2[:, j, :], in_=rawg[b][:, T:T + 2]), reads=[rk], writes=[f"hg2_{j}"])
                ic = ntmp()
                P.add("act", lambda e, j=j, ic=ic, gb=gb: e.activation(
                    out=tmp32[ic][:], in_=psum[gb][:], func=AF.Identity,
                    scale=V("ffn_conv_w", 2 * 24 + j), bias=V("ffn_conv_b", j)),
                    reads=[f"ps{gb}", "vec"], writes=[f"t{ic}"])
                for k in range(2):
                    P.add("dve", lambda e, j=j, k=k, b=b, ic=ic: e.scalar_tensor_tensor(
                        out=tmp32[ic][:], in0=rawg[b][:, k:k + T], scalar=V("ffn_conv_w", k * 24 + j), in1=tmp32[ic][:],
                        op0=ALU.mult, op1=ALU.add),
                        reads=[rk, "vec", f"t{ic}"], writes=[f"t{ic}"])
                P.add("act", lambda e, ic=ic: e.activation(out=tmp32[ic][:], in_=tmp32[ic][:], func=AF.Gelu_apprx_tanh),
                      reads=[f"t{ic}"], writes=[f"t{ic}"])
                P.add("dve", lambda e, j=j, ic=ic, ub=ub: e.tensor_tensor(
                    out=big[:, j, :], in0=psum[ub][:], in1=tmp32[ic][:], op=ALU.mult),
                    reads=[f"ps{ub}", f"t{ic}"], writes=[f"big{j}"])
            for ob_ in range(2):
                banks = [nbank() for _ in range(4)]
                for kp in range(3):
                    bi = 25 + ob_ * 3 + kp
                    keys = [f"big{kp * 8 + k}" for k in range(8)]
                    for oc in range(4):
                        wgroup(t, bi, oc * 128, lambda k, kp=kp: big[:, kp * 8 + k, :], keys, bank=banks[oc],
                               start=(kp == 0), stop=(kp == 2))
                    done_block(t, bi)
                for oc in range(4):
                    evac_with_stats(banks[oc], ob_ * 4 + oc)
            postnorm_resid(par, "g_ffn_post")

            prenorm(par, "g_ple_gate")
            for k in range(2):
                P.add("pool", lambda e, k=k: e.tensor_copy(out=pb[:, k, :], in_=pf[0][:, k, :]),
                      reads=["pf0"], writes=[f"pb{k}"])
            load_p(t + 1)
            for c in range(NCH):
                pbk = nbank()

                def fn(e, c=c, pbk=pbk):
                    ins = None
                    for k in range(2):
                        ins = e.matmul(psum[pbk][:], wpp[:, k, c * 128:(c + 1) * 128], pb[:, k, :], start=(k == 0), stop=(k == 1))
                    return ins
                P.add("pe", fn, reads=["wpp", "pb0", "pb1"], writes=[f"ps{pbk}"])
                evac_with_stats(pbk, c)
            ir = norm_rstd()
            for c in range(NCH):
                bi = 31 + c // 4
                zb = wgroup(t, bi, (c % 4) * 128, hbf_rhs, hbf_keys)
                if c % 4 == 3:
                    done_block(t, bi)
                itg, ipl = ntmp(), ntmp()
                P.add("act", lambda e, itg=itg, zb=zb: e.activation(out=tmp32[itg][:], in_=psum[zb][:], func=AF.Tanh, scale=0.5),
                      reads=[f"ps{zb}"], writes=[f"t{itg}"])
                P.add("dve", lambda e, c=c, ipl=ipl: e.scalar_tensor_tensor(
                    out=tmp32[ipl][:], in0=mos[:, c, :], scalar=V("g_ple_post", c), in1=rstd_t[ir][:], op0=ALU.mult, op1=ALU.mult),
                    reads=[f"mos{c}", "vec", f"rstd{ir}"], writes=[f"t{ipl}"])
                P.add("dve", lambda e, itg=itg, ipl=ipl: e.scalar_tensor_tensor(
                    out=tmp32[ipl][:], in0=tmp32[itg][:], scalar=1.0, in1=tmp32[ipl][:], op0=ALU.add, op1=ALU.mult),
                    reads=[f"t{itg}", f"t{ipl}"], writes=[f"t{ipl}"])
                P.add("dve", lambda e, c=c, ipl=ipl: e.scalar_tensor_tensor(
                    out=xres[par][:, c, :], in0=tmp32[ipl][:], scalar=0.5, in1=xres[par][:, c, :], op0=ALU.mult, op1=ALU.add),
                    reads=[f"t{ipl}", f"x{par}_{c}"], writes=[f"x{par}_{c}"])
            P.add("pool", lambda e, t=t: e.dma_start(out=yT_v[:, :, t * T:(t + 1) * T], in_=xres[par][:]),
                  reads=xkeys[par], writes=[f"y{t}"], chan=f"st{par}")

        for t in range(ntiles):
            emit_tile(t)

        P.add("pool", lambda e: None, reads=[f"y{t}" for t in range(ntiles)])

        print("sbuf bytes remaining:", nc.sbuf_bytes_remaining, "ops:", len(P.ops))
        blk = st.enter_context(nc.Block())
        P.emit(nc, blk, st)
    return nc


def _blk(w, b):
    K = w.shape[0]
    kc = K // 128
    sub = w[:, b * 512:(b + 1) * 512].reshape(kc, 128, 512).transpose(1, 0, 2)
    return sub.reshape(128, kc * 512)


def _prep_weights(inp):
    w_in = inp["w_in"][0]
    w_rg_out = inp["w_rg_out"][0]
    w_o = inp["w_o"][0]
    w_up = inp["w_up"][0]
    w_down = inp["w_down"][0]
    w_pg = inp["w_ple_gate"][0]
    blocks = []
    for b in (0, 5, 6, 1, 2, 3, 4):
        blocks.append(_blk(w_in, b))
    blocks.append(_blk(w_rg_out, 0))
    blocks.append(_blk(w_in, 7))
    blocks.append(_blk(w_rg_out, 1))
    blocks.append(_blk(w_in, 8))
    blocks.append(_blk(w_o, 0))
    blocks.append(_blk(w_o, 1))
    for q in range(6):
        blocks.append(_blk(w_up, q))
        blocks.append(_blk(w_up, 6 + q))
    for ob in range(2):
        for kp in range(3):
            blocks.append(_blk(w_down[kp * 1024:(kp + 1) * 1024], ob))
    blocks.append(_blk(w_pg, 0))
    blocks.append(_blk(w_pg, 1))
    assert len(blocks) == NBLK
    wstream = np.ascontiguousarray(np.stack(blocks, 0), dtype=np.float32)

    def cols(v, n):
        return np.asarray(v, np.float32).reshape(n, 128).T

    vecs = np.zeros((128, NV), np.float32)

    def put(name, arr):
        vecs[:, VOFF[name]:VOFF[name] + arr.shape[1]] = arr
    put("g_mix_pre", cols(inp["g_mix_pre"][0], 8))
    put("g_mix_post", cols(inp["g_mix_post"][0], 8))
    put("pool_scale", cols(inp["pool_scale"][0], 4))
    put("conv_w", np.concatenate([cols(inp["conv_w"][0][k], 8) for k in range(4)], 1))
    put("conv_b", cols(inp["conv_b"][0], 8))
    put("b_r", cols(inp["b_rg_gates"][0][0], 8))
    put("b_i", cols(inp["b_rg_gates"][0][1], 8))
    put("lam", cols(inp["lru_lambda"][0], 8))
    put("g_ffn_pre", cols(inp["g_ffn_pre"][0], 8))
    put("g_ffn_post", cols(inp["g_ffn_post"][0], 8))
    put("ffn_conv_w", np.concatenate([cols(inp["ffn_conv_w"][0][k], 24) for k in range(3)], 1))
    put("ffn_conv_b", cols(inp["ffn_conv_b"][0], 24))
    put("g_ple_gate", cols(inp["g_ple_gate"][0], 8))
    put("g_ple_post", cols(inp["g_ple_post"][0], 8))

    pool_w = np.asarray(inp["pool_w"][0], np.float32)
    poolw = np.ascontiguousarray(pool_w.transpose(1, 0, 2).reshape(128, 512))
    wpo = np.ascontiguousarray(np.asarray(inp["w_pool_out"][0], np.float32).reshape(4, 128, 1024).transpose(1, 0, 2).reshape(128, 4096))
    wpp = np.ascontiguousarray(np.asarray(inp["w_ple_proj"][0], np.float32).reshape(2, 128, 1024).transpose(1, 0, 2).reshape(128, 2048))
    wg = np.asarray(inp["w_rg_gates"][0], np.float32)
    bd = np.zeros((128, 2, 8, 128), np.float32)
    for g in range(2):
        for c in range(8):
            bd[0:64, g, c, 0:64] = wg[g, 2 * c]
            bd[64:128, g, c, 64:128] = wg[g, 2 * c + 1]
    bd = np.ascontiguousarray(bd.reshape(128, 2048))
    return dict(wstream=wstream, vecs=vecs, poolw=poolw, wpo=wpo, wpp=wpp, bd=bd)


_NC_CACHE = {}


def kernel(**inputs):
    inp = {k: np.asarray(v) for k, v in inputs.items()}
    x = inp["x"].astype(np.float32, copy=False)
    p = inp["p"].astype(np.float32, copy=False)
    shared = _prep_weights(inp)
    n = 8
    in_maps = []
    for b in range(n):
        m = dict(shared)
        m["xT"] = np.ascontiguousarray(x[b].T)
        m["pT"] = np.ascontiguousarray(p[0, b].T)
        in_maps.append(m)
    if "nc" not in _NC_CACHE:
        _NC_CACHE["nc"] = build_nc()
    nc = _NC_CACHE["nc"]
    res = run_bass_kernel_spmd(nc, in_maps, core_ids=list(range(n)))
    out = np.stack([np.ascontiguousarray(res.results[b]["yT"].T) for b in range(n)], 0)
    return out.astype(np.float32, copy=False)
```

```python
import contextlib
import numpy as np
import concourse.bass as bass
import concourse.mybir as mybir
from concourse.bass_utils import run_bass_kernel_spmd

F32 = mybir.dt.float32
BF16 = mybir.dt.bfloat16
AF = mybir.ActivationFunctionType
ALU = mybir.AluOpType

D = 1024
S = 4096
T = 512
NT_TILES = S // T
NCH = D // 128
DFF = 3072
NJ = DFF // 128
NBLK = 33
NR = 4
NTMP = 10
EPS = 1e-6
SAME_ENGINE_SYNC = True

VOFF = {}
_o = 0
for _n, _k in [("g_mix_pre", 8), ("g_mix_post", 8), ("pool_scale", 4), ("conv_w", 32), ("conv_b", 8),
               ("b_r", 8), ("b_i", 8), ("lam", 8), ("g_ffn_pre", 8), ("g_ffn_post", 8),
               ("ffn_conv_w", 72), ("ffn_conv_b", 24), ("g_ple_gate", 8), ("g_ple_post", 8)]:
    VOFF[_n] = _o
    _o += _k
NV = _o


class Prog:
    def __init__(self):
        self.ops = []
        self.last_w = {}
        self.readers = {}
        self.chan_count = {}

    def add(self, eng, fn, reads=(), writes=(), chan=None):
        idx = len(self.ops)
        deps = set()
        for r in reads:
            w = self.last_w.get(r)
            if w is not None:
                deps.add(w)
        for k in writes:
            w = self.last_w.get(k)
            if w is not None:
                deps.add(w)
            for rd in self.readers.get(k, ()):
                deps.add(rd)
        for r in reads:
            self.readers.setdefault(r, []).append(idx)
        for k in writes:
            self.last_w[k] = idx
            self.readers[k] = []
        op = dict(eng=eng, fn=fn, chan=chan, deps=deps, marked=False, idx=idx, seq=0, chan_val=0)
        if chan is not None:
            self.chan_count[chan] = self.chan_count.get(chan, 0) + 16
            op["chan_val"] = self.chan_count[chan]
        self.ops.append(op)
        return idx

    def emit(self, nc, block, stack):
        ops = self.ops
        for op in ops:
            for d in list(op["deps"]):
                dop = ops[d]
                if dop["chan"] is not None:
                    continue
                if op["chan"] is None and dop["eng"] == op["eng"]:
                    if op["eng"] == "pe" or not SAME_ENGINE_SYNC:
                        op["deps"].discard(d)
                        continue
                dop["marked"] = True
        cnt = {}
        for op in ops:
            if op["chan"] is None and op["marked"]:
                cnt[op["eng"]] = cnt.get(op["eng"], 0) + 1
                op["seq"] = cnt[op["eng"]]
        esem = {e: stack.enter_context(nc.semaphore("e_" + e)) for e in ["pe", "act", "dve", "pool"]}
        csem = {c: stack.enter_context(nc.semaphore("c_" + c)) for c in self.chan_count}
        streams = {e: [op for op in ops if op["eng"] == e] for e in ["pe", "act", "dve", "pool", "sp"]}

        def run(engname, eng):
            waited = {}
            for op in streams[engname]:
                need = {}
                for d in op["deps"]:
                    dop = ops[d]
                    if dop["chan"] is not None:
                        key, sem, val = "c_" + dop["chan"], csem[dop["chan"]], dop["chan_val"]
                    else:
                        key, sem, val = "e_" + dop["eng"], esem[dop["eng"]], dop["seq"]
                    if val > need.get(key, (None, 0))[1]:
                        need[key] = (sem, val)
                for key, (sem, val) in need.items():
                    if val > waited.get(key, 0):
                        eng.wait_ge(sem, val)
                        waited[key] = val
                ins = op["fn"](eng)
                if ins is None:
                    continue
                if op["chan"] is not None:
                    ins.then_inc(csem[op["chan"]], 16)
                elif op["marked"]:
                    ins.then_inc(esem[engname], 1)

        @block.sync
        def _(e):
            run("sp", e)

        @block.gpsimd
        def _(e):
            run("pool", e)

        @block.scalar
        def _(e):
            run("act", e)

        @block.vector
        def _(e):
            run("dve", e)

        @block.tensor
        def _(e):
            run("pe", e)


def build_nc(ntiles=NT_TILES):
    nc = bass.Bass("TRN2", target_bir_lowering=False)
    xT = nc.dram_tensor("xT", [D, S], F32, kind="ExternalInput").ap()
    pT = nc.dram_tensor("pT", [256, S], F32, kind="ExternalInput").ap()
    vecs_d = nc.dram_tensor("vecs", [128, NV], F32, kind="ExternalInput").ap()
    wstream = nc.dram_tensor("wstream", [NBLK, 128, 4096], F32, kind="ExternalInput").ap()
    poolw_d = nc.dram_tensor("poolw", [128, 512], F32, kind="ExternalInput").ap()
    wpo_d = nc.dram_tensor("wpo", [128, 4096], F32, kind="ExternalInput").ap()
    wpp_d = nc.dram_tensor("wpp", [128, 2048], F32, kind="ExternalInput").ap()
    bd_d = nc.dram_tensor("bd", [128, 2048], F32, kind="ExternalInput").ap()
    scr = nc.dram_tensor("scr", [NBLK, 128, 4096], BF16, kind="Internal").ap()
    yT = nc.dram_tensor("yT", [D, S], F32, kind="ExternalOutput").ap()

    xT_v = xT.rearrange("(c p) t -> p c t", p=128)
    yT_v = yT.rearrange("(c p) t -> p c t", p=128)
    pT_v = pT.rearrange("(c p) t -> p c t", p=128)

    P = Prog()
    with contextlib.ExitStack() as st:
        def sb(name, shape, dt):
            return st.enter_context(nc.sbuf_tensor(name, shape, dt))

        xres = [sb(f"xres{i}", [128, NCH, T], F32) for i in range(2)]
        hbf = sb("hbf", [128, NCH, T], BF16)
        big = sb("big", [128, NJ, T], BF16)
        mos = sb("mos", [128, NCH, T], F32)
        tmp32 = [sb(f"t32_{i}", [128, T], F32) for i in range(NTMP)]
        sqb = [sb(f"sq{i}", [128, T], BF16) for i in range(2)]
        rstd_t = [sb(f"rstd{i}", [128, T], F32) for i in range(2)]
        xcb = [sb(f"xcb{i}", [128, T], BF16) for i in range(2)]
        pb = sb("pb", [128, 2, T], BF16)
        pf = [sb(f"pf{i}", [128, 2, T], F32) for i in range(1)]
        rawp = [sb(f"rawp{g}", [128, T + 16], F32) for g in range(2)]
        hp = sb("hp", [128, 4, 16], F32)
        sA = sb("sA", [128, T + 16], F32)
        sB = sb("sB", [128, T + 16], F32)
        rawx = [sb(f"rawx{i}", [128, T + 3], F32) for i in range(2)]
        rawg = [sb(f"rawg{i}", [128, T + 2], F32) for i in range(2)]
        hx = sb("hx", [128, NCH, 3], F32)
        hg2 = sb("hg2", [128, NJ, 2], F32)
        hstate = sb("hstate", [128, NCH], F32)
        vec = sb("vec", [128, NV], F32)
        dv = sb("dv", [128, 64], F32)
        invc = sb("invc", [128, 4, 16], F32)
        t16 = sb("t16", [128, 16], F32)
        ones = sb("ones", [128, 128], BF16)
        ring = [sb(f"ring{i}", [128, 8, 512], BF16) for i in range(NR)]
        poolw = sb("poolw_s", [128, 4, 128], BF16)
        wpo = sb("wpo_s", [128, 4, 1024], BF16)
        wpp = sb("wpp_s", [128, 2, 1024], BF16)
        bd = sb("bd_s", [128, 2, NCH, 128], BF16)
        psum = [st.enter_context(nc.psum_tensor(f"ps{i}", [128, T], F32)) for i in range(8)]
        NRM = 7

        state = {"bank": 0, "tmp": 0, "rstd": 0}

        def nbank():
            b = state["bank"]
            state["bank"] = (b + 1) % 7
            return b

        def ntmp():
            i = state["tmp"]
            state["tmp"] = (i + 1) % NTMP
            return i

        def V(name, col):
            o = VOFF[name] + col
            return vec[:, o:o + 1]

        DV = {"neg_br": 0, "neg_bi": 8, "c8": 16, "c16": 24, "ps_half": 32, "e": 40, "sp": 48}

        def DVc(name, col):
            o = DV[name] + col
            return dv[:, o:o + 1]

        P.add("sp", lambda e: e.dma_start(out=vec[:], in_=vecs_d[:, :]), writes=["vec"], chan="vec")
        P.add("pool", lambda e: e.dma_start(out=poolw[:].rearrange("p a b -> p (a b)"), in_=poolw_d[:, :]),
              writes=["poolw"], chan="w_poolw")
        P.add("pool", lambda e: e.dma_start(out=wpo[:].rearrange("p a b -> p (a b)"), in_=wpo_d[:, :]),
              writes=["wpo"], chan="w_wpo")
        P.add("pool", lambda e: e.dma_start(out=wpp[:].rearrange("p a b -> p (a b)"), in_=wpp_d[:, :]),
              writes=["wpp"], chan="w_wpp")
        P.add("pool", lambda e: e.dma_start(out=bd[:].rearrange("p a b c -> p (a b c)"), in_=bd_d[:, :]),
              writes=["bd"], chan="w_bd")
        P.add("pool", lambda e: e.dma_start(out=xres[0][:], in_=xT_v[:, :, 0:T]),
              writes=[f"x0_{c}" for c in range(NCH)], chan="x0")
        P.add("pool", lambda e: e.dma_start(out=pf[0][:], in_=pT_v[:, :, 0:T]), writes=["pf0"], chan="p0")
        cast_groups = [(0, 3), (3, 7), (7, 13), (13, 19), (19, 25), (25, 29), (29, 33)]
        for gi, (a, b) in enumerate(cast_groups):
            P.add("pool", lambda e, a=a, b=b: e.dma_start(out=scr[a:b], in_=wstream[a:b]),
                  writes=[f"scr{i}" for i in range(a, b)], chan=f"cast{gi}")

        P.add("dve", lambda e: e.memset(ones[:], 1.0), writes=["ones"])
        P.add("dve", lambda e: e.memset(hstate[:], 0.0), writes=[f"hstate{c}" for c in range(NCH)])
        P.add("dve", lambda e: e.memset(hx[:].rearrange("p a b -> p (a b)"), 0.0), writes=[f"hx{c}" for c in range(NCH)])
        P.add("dve", lambda e: e.memset(hg2[:].rearrange("p a b -> p (a b)"), 0.0), writes=[f"hg2_{j}" for j in range(NJ)])
        P.add("dve", lambda e: e.memset(hp[:].rearrange("p a b -> p (a b)"), 0.0), writes=[f"hp{g}" for g in range(4)])
        for g, w in enumerate((2, 4, 8, 16)):
            P.add("dve", lambda e, g=g, w=w: e.memset(invc[:, g, :], 1.0 / w), writes=["invc"])
            for t in range(w - 1):
                P.add("dve", lambda e, g=g, t=t: e.memset(invc[:, g, t:t + 1], 1.0 / (t + 1)), writes=["invc"])
        P.add("dve", lambda e: e.tensor_scalar(out=dv[:, 0:8], in0=vec[:, VOFF["b_r"]:VOFF["b_r"] + 8],
                                               scalar1=-1.0, scalar2=None, op0=ALU.mult),
              reads=["vec"], writes=["dv_br"])
        P.add("dve", lambda e: e.tensor_scalar(out=dv[:, 8:16], in0=vec[:, VOFF["b_i"]:VOFF["b_i"] + 8],
                                               scalar1=-1.0, scalar2=None, op0=ALU.mult),
              reads=["vec"], writes=["dv_bi"])
        P.add("dve", lambda e: e.tensor_scalar(out=dv[:, 32:36], in0=vec[:, VOFF["pool_scale"]:VOFF["pool_scale"] + 4],
                                               scalar1=0.5, scalar2=None, op0=ALU.mult),
              reads=["vec"], writes=["dv_ps"])
        P.add("act", lambda e: e.activation(out=dv[:, 40:48], in_=vec[:, VOFF["lam"]:VOFF["lam"] + 8],
                                            func=AF.Exp, scale=-1.0),
              reads=["vec"], writes=["dv_e"])
        P.add("act", lambda e: e.activation(out=dv[:, 48:56], in_=dv[:, 40:48], func=AF.Ln, bias=1.0),
              reads=["dv_e"], writes=["dv_sp"])
        P.add("dve", lambda e: e.tensor_scalar(out=dv[:, 16:24], in0=dv[:, 48:56], scalar1=-8.0, scalar2=None,
                                               op0=ALU.mult), reads=["dv_sp"], writes=["dv_c8"])
        P.add("dve", lambda e: e.tensor_scalar(out=dv[:, 24:32], in0=dv[:, 48:56], scalar1=-16.0, scalar2=None,
                                               op0=ALU.mult), reads=["dv_sp"], writes=["dv_hc8"])
        DVKEYS = ["dv_br", "dv_bi", "dv_ps", "dv_c8", "dv_hc8"]

        total_blocks = ntiles * NBLK

        def load_block(gi):
            if gi >= total_blocks:
                return
            slot = gi % NR
            bi = gi % NBLK
            P.add("sp", lambda e, slot=slot, bi=bi: e.dma_start(
                out=ring[slot][:].rearrange("p a b -> p (a b)"), in_=scr[bi]),
                reads=[f"scr{bi}"], writes=[f"ring{slot}"], chan=f"ring{slot}")

        for gi in range(NR):
            load_block(gi)

        def wgroup(t, bi, col, rhs_fn, rhs_keys, nk=8, bank=None, start=True, stop=True, extra_reads=()):
            gi = t * NBLK + bi
            slot = gi % NR
            if bank is None:
                bank = nbank()

            def fn(e, slot=slot, bank=bank):
                ins = None
                for k in range(nk):
                    ins = e.matmul(psum[bank][:], ring[slot][:, k, col:col + 128], rhs_fn(k),
                                   start=(start and k == 0), stop=(stop and k == nk - 1))
                return ins
            P.add("pe", fn, reads=[f"ring{slot}"] + list(rhs_keys) + list(extra_reads), writes=[f"ps{bank}"])
            return bank

        def done_block(t, bi):
            load_block(t * NBLK + bi + NR)

        hbf_keys = [f"hbf{k}" for k in range(NCH)]

        def hbf_rhs(k):
            return hbf[:, k, :]

        def norm_stats_chunk(src_ap, src_key, c, n=NCH):
            b = c % 2
            P.add("act", lambda e, b=b: e.activation(out=sqb[b][:], in_=src_ap, func=AF.Square),
                  reads=[src_key], writes=[f"sq{b}"])
            P.add("pe", lambda e, b=b: e.matmul(psum[NRM][:], ones[:], sqb[b][:], start=(c == 0), stop=(c == n - 1)),
                  reads=["ones", f"sq{b}"], writes=[f"ps{NRM}"])

        def norm_rstd():
            i2 = state["rstd"]
            state["rstd"] = 1 - i2
            P.add("act", lambda e: e.activation(out=rstd_t[i2][:], in_=psum[NRM][:], func=AF.Ln, scale=1.0 / D, bias=EPS),
                  reads=[f"ps{NRM}"], writes=[f"rstd{i2}"])
            P.add("act", lambda e: e.activation(out=rstd_t[i2][:], in_=rstd_t[i2][:], func=AF.Exp, scale=-0.5),
                  reads=[f"rstd{i2}"], writes=[f"rstd{i2}"])
            return i2

        def prenorm(par, gname):
            for c in range(NCH):
                norm_stats_chunk(xres[par][:, c, :], f"x{par}_{c}", c)
            ir = norm_rstd()
            for c in range(NCH):
                P.add("dve", lambda e, c=c: e.scalar_tensor_tensor(
                    out=hbf[:, c, :], in0=xres[par][:, c, :], scalar=V(gname, c), in1=rstd_t[ir][:],
                    op0=ALU.mult, op1=ALU.mult),
                    reads=[f"x{par}_{c}", "vec", f"rstd{ir}"], writes=[f"hbf{c}"])

        def postnorm_resid(par, gname):
            ir = norm_rstd()
            for c in range(NCH):
                it = ntmp()
                P.add("dve", lambda e, c=c, it=it: e.scalar_tensor_tensor(
                    out=tmp32[it][:], in0=mos[:, c, :], scalar=V(gname, c), in1=rstd_t[ir][:],
                    op0=ALU.mult, op1=ALU.mult),
                    reads=[f"mos{c}", "vec", f"rstd{ir}"], writes=[f"t{it}"])
                P.add("pool", lambda e, c=c, it=it: e.tensor_tensor(
                    out=xres[par][:, c, :], in0=tmp32[it][:], in1=xres[par][:, c, :], op=ALU.add),
                    reads=[f"t{it}", f"x{par}_{c}"], writes=[f"x{par}_{c}"])

        def evac_with_stats(bank, c):
            norm_stats_chunk(psum[bank][:], f"ps{bank}", c)
            P.add("act", lambda e: e.activation(out=mos[:, c, :], in_=psum[bank][:], func=AF.Copy),
                  reads=[f"ps{bank}"], writes=[f"mos{c}"])

        xkeys = [[f"x{par}_{c}" for c in range(NCH)] for par in range(2)]

        def load_p(t):
            if t >= ntiles:
                return
            P.add("pool", lambda e: e.dma_start(out=pf[0][:], in_=pT_v[:, :, t * T:(t + 1) * T]),
                  writes=["pf0"], chan="p0")

        def load_x(t):
            if t >= ntiles:
                return
            par = t % 2
            P.add("pool", lambda e: e.dma_start(out=xres[par][:], in_=xT_v[:, :, t * T:(t + 1) * T]),
                  writes=xkeys[par], chan=f"x{par}")

        def emit_tile(t):
            par = t % 2
            HG, MP, MR = 0, 8, 16
            DD, PM = 16, 20

            prenorm(par, "g_mix_pre")
            load_x(t + 1)

            for g, w in enumerate((2, 4, 8, 16)):
                bank = wgroup(t, 0, g * 128, hbf_rhs, hbf_keys)
                rb_ = g % 2
                rk = f"rawp{rb_}"
                rp = rawp[rb_]
                P.add("pool", lambda e, g=g, rp=rp: e.tensor_copy(out=rp[:, 0:16], in_=hp[:, g, :]), reads=[f"hp{g}"], writes=[rk])
                P.add("act", lambda e, rp=rp, bank=bank: e.activation(out=rp[:, 16:16 + T], in_=psum[bank][:], func=AF.Copy),
                      reads=[f"ps{bank}"], writes=[rk])
                P.add("pool", lambda e, g=g, rp=rp: e.tensor_copy(out=hp[:, g, :], in_=rp[:, T:T + 16]), reads=[rk], writes=[f"hp{g}"])
                L = T + 16
                P.add("pool", lambda e, rp=rp: e.tensor_tensor(out=sA[:, 1:L], in0=rp[:, 1:L], in1=rp[:, 0:L - 1], op=ALU.add),
                      reads=[rk], writes=["sA"])
                last, lastk = sA, "sA"
                if w >= 4:
                    P.add("pool", lambda e: e.tensor_tensor(out=sB[:, 3:L], in0=sA[:, 3:L], in1=sA[:, 1:L - 2], op=ALU.add),
                          reads=["sA"], writes=["sB"])
                    last, lastk = sB, "sB"
                if w >= 8:
                    P.add("pool", lambda e: e.tensor_tensor(out=sA[:, 7:L], in0=sB[:, 7:L], in1=sB[:, 3:L - 4], op=ALU.add),
                          reads=["sB"], writes=["sA"])
                    last, lastk = sA, "sA"
                if w >= 16:
                    P.add("pool", lambda e: e.tensor_tensor(out=sB[:, 15:L], in0=sA[:, 15:L], in1=sA[:, 7:L - 8], op=ALU.add),
                          reads=["sA"], writes=["sB"])
                    last, lastk = sB, "sB"
                dk = f"big{DD + g}"
                P.add("dve", lambda e, g=g, w=w, last=last, rp=rp: e.scalar_tensor_tensor(
                    out=big[:, DD + g, :], in0=last[:, 16:16 + T], scalar=1.0 / w, in1=rp[:, 16:16 + T],
                    op0=ALU.mult, op1=ALU.subtract),
                    reads=[lastk, rk], writes=[dk])
                if t == 0:
                    P.add("dve", lambda e, g=g, last=last: e.tensor_tensor(out=t16[:], in0=last[:, 16:32], in1=invc[:, g, :], op=ALU.mult),
                          reads=[lastk, "invc", dk], writes=["t16"])
                    P.add("dve", lambda e, g=g, rp=rp: e.tensor_tensor(out=big[:, DD + g, 0:16], in0=t16[:], in1=rp[:, 16:32], op=ALU.subtract),
                          reads=["t16", rk], writes=[dk])
                bk2 = nbank()
                P.add("pe", lambda e, g=g, bk2=bk2: e.matmul(psum[bk2][:], poolw[:, g, :], big[:, DD + g, :], start=True, stop=True),
                      reads=["poolw", dk], writes=[f"ps{bk2}"])
                P.add("act", lambda e, g=g, bk2=bk2: e.activation(out=big[:, PM + g, :], in_=psum[bk2][:], func=AF.Identity,
                                                                 scale=DVc("ps_half", g)),
                      reads=[f"ps{bk2}", "dv_ps"], writes=[f"big{PM + g}"])
            done_block(t, 0)

            stA = {}

            def rnn_A(c):
                bi = 1 + c // 4
                ub = wgroup(t, bi, (c % 4) * 128, hbf_rhs, hbf_keys)
                if c % 4 == 3:
                    done_block(t, bi)
                b = c % 2
                rk = f"rawx{b}"
                P.add("pool", lambda e: e.tensor_copy(out=rawx[b][:, 0:3], in_=hx[:, c, :]), reads=[f"hx{c}"], writes=[rk])
                P.add("act", lambda e: e.activation(out=rawx[b][:, 3:3 + T], in_=psum[ub][:], func=AF.Copy),
                      reads=[f"ps{ub}"], writes=[rk])
                P.add("pool", lambda e: e.tensor_copy(out=hx[:, c, :], in_=rawx[b][:, T:T + 3]), reads=[rk], writes=[f"hx{c}"])
                ixc = ntmp()
                P.add("act", lambda e: e.activation(out=tmp32[ixc][:], in_=psum[ub][:], func=AF.Identity,
                                                    scale=V("conv_w", 3 * 8 + c), bias=V("conv_b", c)),
                      reads=[f"ps{ub}", "vec"], writes=[f"t{ixc}"])
                for k in range(3):
                    P.add("dve", lambda e, k=k: e.scalar_tensor_tensor(
                        out=tmp32[ixc][:], in0=rawx[b][:, k:k + T], scalar=V("conv_w", k * 8 + c), in1=tmp32[ixc][:],
                        op0=ALU.mult, op1=ALU.add),
                        reads=[rk, "vec", f"t{ixc}"], writes=[f"t{ixc}"])
                P.add("pool", lambda e: e.tensor_copy(out=xcb[b][:], in_=tmp32[ixc][:]), reads=[f"t{ixc}"], writes=[f"xcb{b}"])
                stA[c] = ixc

            def rnn_B(c):
                b = c % 2
                ixc = stA[c]
                rb = nbank()
                P.add("pe", lambda e: e.matmul(psum[rb][:], bd[:, 0, c, :], xcb[b][:], start=True, stop=True),
                      reads=["bd", f"xcb{b}"], writes=[f"ps{rb}"])
                ib = nbank()
                P.add("pe", lambda e: e.matmul(psum[ib][:], bd[:, 1, c, :], xcb[b][:], start=True, stop=True),
                      reads=["bd", f"xcb{b}"], writes=[f"ps{ib}"])
                i1, ia, i2, ii = ntmp(), ntmp(), ntmp(), ntmp()
                k1, ka, k2, ki = f"t{i1}", f"t{ia}", f"t{i2}", f"t{ii}"
                t1, ta, t2, ti = tmp32[i1], tmp32[ia], tmp32[i2], tmp32[ii]
                P.add("act", lambda e: e.activation(out=t1[:], in_=psum[rb][:], func=AF.Exp, scale=-1.0, bias=DVc("neg_br", c)),
                      reads=[f"ps{rb}", "dv_br"], writes=[k1])
                P.add("act", lambda e: e.activation(out=ti[:], in_=psum[ib][:], func=AF.Exp, scale=-1.0, bias=DVc("neg_bi", c)),
                      reads=[f"ps{ib}", "dv_bi"], writes=[ki])
                P.add("act", lambda e: e.activation(out=t1[:], in_=t1[:], func=AF.Ln, bias=1.0), reads=[k1], writes=[k1])
                P.add("act", lambda e: e.activation(out=ti[:], in_=ti[:], func=AF.Ln, bias=1.0), reads=[ki], writes=[ki])
                P.add("act", lambda e: e.activation(out=t1[:], in_=t1[:], func=AF.Exp, scale=-1.0), reads=[k1], writes=[k1])
                P.add("act", lambda e: e.activation(out=ti[:], in_=ti[:], func=AF.Exp, scale=-1.0), reads=[ki], writes=[ki])
                P.add("act", lambda e: e.activation(out=ta[:], in_=t1[:], func=AF.Exp, scale=DVc("c8", c)),
                      reads=[k1, "dv_c8"], writes=[ka])
                P.add("act", lambda e: e.activation(out=t2[:], in_=t1[:], func=AF.Exp, scale=DVc("c16", c)),
                      reads=[k1, "dv_hc8"], writes=[k2])
                P.add("act", lambda e: e.activation(out=t2[:], in_=t2[:], func=AF.Ln, scale=-1.0, bias=1.0), reads=[k2], writes=[k2])
                P.add("act", lambda e: e.activation(out=t2[:], in_=t2[:], func=AF.Exp, scale=0.5), reads=[k2], writes=[k2])
                if t == 0:
                    P.add("dve", lambda e: e.memset(t2[:, 0:1], 1.0), reads=[k2], writes=[k2])
                P.add("dve", lambda e: e.tensor_tensor(out=ti[:], in0=ti[:], in1=tmp32[ixc][:], op=ALU.mult),
                      reads=[ki, f"t{ixc}"], writes=[ki])
                P.add("dve", lambda e: e.tensor_tensor(out=ti[:], in0=ti[:], in1=t2[:], op=ALU.mult),
                      reads=[ki, k2], writes=[ki])
                P.add("dve", lambda e: e.tensor_tensor_scan(out=mos[:, c, :], data0=ta[:], data1=ti[:],
                                                            initial=hstate[:, c:c + 1], op0=ALU.mult, op1=ALU.add),
                      reads=[ka, ki, f"hstate{c}"], writes=[f"mos{c}"])
                P.add("pool", lambda e: e.tensor_copy(out=hstate[:, c:c + 1], in_=mos[:, c, T - 1:T]),
                      reads=[f"mos{c}"], writes=[f"hstate{c}"])

            for i in range(NCH + 1):
                if i < NCH:
                    rnn_A(i)
                if i >= 1:
                    rnn_B(i - 1)

            for c in range(NCH):
                bi = 3 + c // 4
                gb = wgroup(t, bi, (c % 4) * 128, hbf_rhs, hbf_keys)
                if c % 4 == 3:
                    done_block(t, bi)
                ig = ntmp()
                P.add("act", lambda e, ig=ig, gb=gb: e.activation(out=tmp32[ig][:], in_=psum[gb][:], func=AF.Gelu_apprx_tanh),
                      reads=[f"ps{gb}"], writes=[f"t{ig}"])
                P.add("dve", lambda e, c=c, ig=ig: e.scalar_tensor_tensor(
                    out=big[:, HG + c, :], in0=mos[:, c, :], scalar=0.5, in1=tmp32[ig][:], op0=ALU.mult, op1=ALU.mult),
                    reads=[f"mos{c}", f"t{ig}"], writes=[f"big{HG + c}"])

            for c in range(NCH):
                bi = 5 + c // 4
                zb = wgroup(t, bi, (c % 4) * 128, hbf_rhs, hbf_keys)
                if c % 4 == 3:
                    done_block(t, bi)
                yb = nbank()

                def fn(e, c=c, yb=yb):
                    ins = None
                    for g in range(4):
                        ins = e.matmul(psum[yb][:], wpo[:, g, c * 128:(c + 1) * 128], big[:, PM + g, :],
                                       start=(g == 0), stop=(g == 3))
                    return ins
                P.add("pe", fn, reads=["wpo"] + [f"big{PM + g}" for g in range(4)], writes=[f"ps{yb}"])
                it = ntmp()
                P.add("act", lambda e, it=it, zb=zb: e.activation(out=tmp32[it][:], in_=psum[zb][:], func=AF.Tanh, scale=0.5),
                      reads=[f"ps{zb}"], writes=[f"t{it}"])
                P.add("dve", lambda e, c=c, it=it, yb=yb: e.scalar_tensor_tensor(
                    out=big[:, MP + c, :], in0=tmp32[it][:], scalar=1.0, in1=psum[yb][:], op0=ALU.add, op1=ALU.mult),
                    reads=[f"t{it}", f"ps{yb}"], writes=[f"big{MP + c}"])

            hg_keys = [f"big{HG + k}" for k in range(NCH)]
            for c in range(NCH):
                bz = 8 + 2 * (c // 4)
                by = 7 + 2 * (c // 4)
                zb = wgroup(t, bz, (c % 4) * 128, hbf_rhs, hbf_keys)
                yb = wgroup(t, by, (c % 4) * 128, lambda k: big[:, HG + k, :], hg_keys)
                if c % 4 == 3:
                    done_block(t, by)
                    done_block(t, bz)
                it = ntmp()
                P.add("act", lambda e, it=it, zb=zb: e.activation(out=tmp32[it][:], in_=psum[zb][:], func=AF.Tanh, scale=0.5),
                      reads=[f"ps{zb}"], writes=[f"t{it}"])
                P.add("dve", lambda e, c=c, it=it, yb=yb: e.scalar_tensor_tensor(
                    out=big[:, MR + c, :], in0=tmp32[it][:], scalar=1.0, in1=psum[yb][:], op0=ALU.add, op1=ALU.mult),
                    reads=[f"t{it}", f"ps{yb}"], writes=[f"big{MR + c}"])

            mpr_keys = [f"big{MP + k}" for k in range(NCH)] + [f"big{MR + k}" for k in range(NCH)]
            for c in range(NCH):
                bi = 11 + c // 4
                ob = wgroup(t, bi, (c % 4) * 128, lambda k: big[:, MP + k, :], mpr_keys, start=True, stop=False)
                wgroup(t, bi, (c % 4) * 128, lambda k: big[:, MR + k, :], mpr_keys, bank=ob, start=False, stop=True)
                if c % 4 == 3:
                    done_block(t, bi)
                evac_with_stats(ob, c)
            postnorm_resid(par, "g_mix_post")

            prenorm(par, "g_ffn_pre")
            for j in range(NJ):
                q, jj = j // 4, j % 4
                bg = 13 + 2 * q
                bu = 14 + 2 * q
                gb = wgroup(t, bg, jj * 128, hbf_rhs, hbf_keys)
                ub = wgroup(t, bu, jj * 128, hbf_rhs, hbf_keys)
                if jj == 3:
                    done_block(t, bg)
                    done_block(t, bu)
                b = j % 2
                rk = f"rawg{b}"
                P.add("pool", lambda e, j=j, b=b: e.tensor_copy(out=rawg[b][:, 0:2], in_=hg2[:, j, :]), reads=[f"hg2_{j}"], writes=[rk])
                P.add("act", lambda e, b=b, gb=gb: e.activation(out=rawg[b][:, 2:2 + T], in_=psum[gb][:], func=AF.Copy),
                      reads=[f"ps{gb}"], writes=[rk])
                P.add("pool", lambda e, j=j, b=b: e.tensor_copy(out=hg2[:, j, :], in_=rawg[b][:, T:T + 2]), reads=[rk], writes=[f"hg2_{j}"])
                ic = ntmp()
                P.add("act", lambda e, j=j, ic=ic, gb=gb: e.activation(
                    out=tmp32[ic][:], in_=psum[gb][:], func=AF.Identity,
                    scale=V("ffn_conv_w", 2 * 24 + j), bias=V("ffn_conv_b", j)),
                    reads=[f"ps{gb}", "vec"], writes=[f"t{ic}"])
                for k in range(2):
                    P.add("dve", lambda e, j=j, k=k, b=b, ic=ic: e.scalar_tensor_tensor(
                        out=tmp32[ic][:], in0=rawg[b][:, k:k + T], scalar=V("ffn_conv_w", k * 24 + j), in1=tmp32[ic][:],
                        op0=ALU.mult, op1=ALU.add),
                        reads=[rk, "vec", f"t{ic}"], writes=[f"t{ic}"])
                P.add("act", lambda e, ic=ic: e.activation(out=tmp32[ic][:], in_=tmp32[ic][:], func=AF.Gelu_apprx_tanh),
                      reads=[f"t{ic}"], writes=[f"t{ic}"])
                P.add("dve", lambda e, j=j, ic=ic, ub=ub: e.tensor_tensor(
                    out=big[:, j, :], in0=psum[ub][:], in1=tmp32[ic][:], op=ALU.mult),
                    reads=[f"ps{ub}", f"t{ic}"], writes=[f"big{j}"])
            for ob_ in range(2):
                banks = [nbank() for _ in range(4)]
                for kp in range(3):
                    bi = 25 + ob_ * 3 + kp
                    keys = [f"big{kp * 8 + k}" for k in range(8)]
                    for oc in range(4):
                        wgroup(t, bi, oc * 128, lambda k, kp=kp: big[:, kp * 8 + k, :], keys, bank=banks[oc],
                               start=(kp == 0), stop=(kp == 2))
                    done_block(t, bi)
                for oc in range(4):
                    evac_with_stats(banks[oc], ob_ * 4 + oc)
            postnorm_resid(par, "g_ffn_post")

            prenorm(par, "g_ple_gate")
            for k in range(2):
                P.add("pool", lambda e, k=k: e.tensor_copy(out=pb[:, k, :], in_=pf[0][:, k, :]),
                      reads=["pf0"], writes=[f"pb{k}"])
            load_p(t + 1)
            for c in range(NCH):
                pbk = nbank()

                def fn(e, c=c, pbk=pbk):
                    ins = None
                    for k in range(2):
                        ins = e.matmul(psum[pbk][:], wpp[:, k, c * 128:(c + 1) * 128], pb[:, k, :], start=(k == 0), stop=(k == 1))
                    return ins
                P.add("pe", fn, reads=["wpp", "pb0", "pb1"], writes=[f"ps{pbk}"])
                evac_with_stats(pbk, c)
            ir = norm_rstd()
            for c in range(NCH):
                bi = 31 + c // 4
                zb = wgroup(t, bi, (c % 4) * 128, hbf_rhs, hbf_keys)
                if c % 4 == 3:
                    done_block(t, bi)
                itg, ipl = ntmp(), ntmp()
                P.add("act", lambda e, itg=itg, zb=zb: e.activation(out=tmp32[itg][:], in_=psum[zb][:], func=AF.Tanh, scale=0.5),
                      reads=[f"ps{zb}"], writes=[f"t{itg}"])
                P.add("dve", lambda e, c=c, ipl=ipl: e.scalar_tensor_tensor(
                    out=tmp32[ipl][:], in0=mos[:, c, :], scalar=V("g_ple_post", c), in1=rstd_t[ir][:], op0=ALU.mult, op1=ALU.mult),
                    reads=[f"mos{c}", "vec", f"rstd{ir}"], writes=[f"t{ipl}"])
                P.add("dve", lambda e, itg=itg, ipl=ipl: e.scalar_tensor_tensor(
                    out=tmp32[ipl][:], in0=tmp32[itg][:], scalar=1.0, in1=tmp32[ipl][:], op0=ALU.add, op1=ALU.mult),
                    reads=[f"t{itg}", f"t{ipl}"], writes=[f"t{ipl}"])
                P.add("dve", lambda e, c=c, ipl=ipl: e.scalar_tensor_tensor(
                    out=xres[par][:, c, :], in0=tmp32[ipl][:], scalar=0.5, in1=xres[par][:, c, :], op0=ALU.mult, op1=ALU.add),
                    reads=[f"t{ipl}", f"x{par}_{c}"], writes=[f"x{par}_{c}"])
            P.add("pool", lambda e, t=t: e.dma_start(out=yT_v[:, :, t * T:(t + 1) * T], in_=xres[par][:]),
                  reads=xkeys[par], writes=[f"y{t}"], chan=f"st{par}")

        for t in range(ntiles):
            emit_tile(t)

        P.add("pool", lambda e: None, reads=[f"y{t}" for t in range(ntiles)])

        print("sbuf bytes remaining:", nc.sbuf_bytes_remaining, "ops:", len(P.ops))
        blk = st.enter_context(nc.Block())
        P.emit(nc, blk, st)
    return nc


def _blk(w, b):
    K = w.shape[0]
    kc = K // 128
    sub = w[:, b * 512:(b + 1) * 512].reshape(kc, 128, 512).transpose(1, 0, 2)
    return sub.reshape(128, kc * 512)


def _prep_weights(inp):
    w_in = inp["w_in"][0]
    w_rg_out = inp["w_rg_out"][0]
    w_o = inp["w_o"][0]
    w_up = inp["w_up"][0]
    w_down = inp["w_down"][0]
    w_pg = inp["w_ple_gate"][0]
    blocks = []
    for b in (0, 1, 2, 3, 4, 5, 6):
        blocks.append(_blk(w_in, b))
    blocks.append(_blk(w_rg_out, 0))
    blocks.append(_blk(w_in, 7))
    blocks.append(_blk(w_rg_out, 1))
    blocks.append(_blk(w_in, 8))
    blocks.append(_blk(w_o, 0))
    blocks.append(_blk(w_o, 1))
    for q in range(6):
        blocks.append(_blk(w_up, q))
        blocks.append(_blk(w_up, 6 + q))
    for ob in range(2):
        for kp in range(3):
            blocks.append(_blk(w_down[kp * 1024:(kp + 1) * 1024], ob))
    blocks.append(_blk(w_pg, 0))
    blocks.append(_blk(w_pg, 1))
    assert len(blocks) == NBLK
    wstream = np.ascontiguousarray(np.stack(blocks, 0), dtype=np.float32)

    def cols(v, n):
        return np.asarray(v, np.float32).reshape(n, 128).T

    vecs = np.zeros((128, NV), np.float32)

    def put(name, arr):
        vecs[:, VOFF[name]:VOFF[name] + arr.shape[1]] = arr
    put("g_mix_pre", cols(inp["g_mix_pre"][0], 8))
    put("g_mix_post", cols(inp["g_mix_post"][0], 8))
    put("pool_scale", cols(inp["pool_scale"][0], 4))
    put("conv_w", np.concatenate([cols(inp["conv_w"][0][k], 8) for k in range(4)], 1))
    put("conv_b", cols(inp["conv_b"][0], 8))
    put("b_r", cols(inp["b_rg_gates"][0][0], 8))
    put("b_i", cols(inp["b_rg_gates"][0][1], 8))
    put("lam", cols(inp["lru_lambda"][0], 8))
    put("g_ffn_pre", cols(inp["g_ffn_pre"][0], 8))
    put("g_ffn_post", cols(inp["g_ffn_post"][0], 8))
    put("ffn_conv_w", np.concatenate([cols(inp["ffn_conv_w"][0][k], 24) for k in range(3)], 1))
    put("ffn_conv_b", cols(inp["ffn_conv_b"][0], 24))
    put("g_ple_gate", cols(inp["g_ple_gate"][0], 8))
    put("g_ple_post", cols(inp["g_ple_post"][0], 8))

    pool_w = np.asarray(inp["pool_w"][0], np.float32)
    poolw = np.ascontiguousarray(pool_w.transpose(1, 0, 2).reshape(128, 512))
    wpo = np.ascontiguousarray(np.asarray(inp["w_pool_out"][0], np.float32).reshape(4, 128, 1024).transpose(1, 0, 2).reshape(128, 4096))
    wpp = np.ascontiguousarray(np.asarray(inp["w_ple_proj"][0], np.float32).reshape(2, 128, 1024).transpose(1, 0, 2).reshape(128, 2048))
    wg = np.asarray(inp["w_rg_gates"][0], np.float32)
    bd = np.zeros((128, 2, 8, 128), np.float32)
    for g in range(2):
        for c in range(8):
            bd[0:64, g, c, 0:64] = wg[g, 2 * c]
            bd[64:128, g, c, 64:128] = wg[g, 2 * c + 1]
    bd = np.ascontiguousarray(bd.reshape(128, 2048))
    return dict(wstream=wstream, vecs=vecs, poolw=poolw, wpo=wpo, wpp=wpp, bd=bd)


_NC_CACHE = {}


def kernel(**inputs):
    inp = {k: np.asarray(v) for k, v in inputs.items()}
    x = inp["x"].astype(np.float32, copy=False)
    p = inp["p"].astype(np.float32, copy=False)
    shared = _prep_weights(inp)
    n = 8
    in_maps = []
    for b in range(n):
        m = dict(shared)
        m["xT"] = np.ascontiguousarray(x[b].T)
        m["pT"] = np.ascontiguousarray(p[0, b].T)
        in_maps.append(m)
    if "nc" not in _NC_CACHE:
        _NC_CACHE["nc"] = build_nc()
    nc = _NC_CACHE["nc"]
    res = run_bass_kernel_spmd(nc, in_maps, core_ids=list(range(n)))
    out = np.stack([np.ascontiguousarray(res.results[b]["yT"].T) for b in range(n)], 0)
    return out.astype(np.float32, copy=False)
```
